# Optimizing a Trainium2 kernel written in Bass

```python
import jax, jax.numpy as jnp
from jax import lax
import numpy as np

D_MODEL = 2048
BATCH = 2
SEQ = 8192
DEPTH = 1

HEAD_DIM = 128
ATTN_GROUPS = ((128, 1), (512, 4), (2048, 16))
N_GROUPS = 3
ATTN_HEADS_PER_GROUP = 8
ATTN_QKV_WIDTH = N_GROUPS * ATTN_HEADS_PER_GROUP * HEAD_DIM
ATTN_OUT_WIDTH = ATTN_HEADS_PER_GROUP * HEAD_DIM
HGRN_HEADS = 8
HGRN_KEY_DIM = 128
HGRN_VAL_DIM = 128
HGRN_WIDTH = HGRN_HEADS * HGRN_KEY_DIM
HGRN_CHUNK = 64
ROPE_THETA = 10000.0
NORM_EPS = 1e-6
D_FF = -(-8 * D_MODEL // (3 * 256)) * 256
IN_COLS = 3 * ATTN_QKV_WIDTH + 4 * HGRN_WIDTH + 2 * D_MODEL

kernel_name = "hybrid_dilated_attn_hgrn2_gated_block"


def rmsnorm(x, gain):
    xf = x.astype(jnp.float32)
    y = xf * lax.rsqrt(jnp.mean(xf * xf, axis=-1, keepdims=True) + NORM_EPS)
    return (y * gain.astype(jnp.float32)).astype(x.dtype)


def rotary(t, seq_len):
    inv_freq = ROPE_THETA ** (-jnp.arange(0, HEAD_DIM, 2, dtype=jnp.float32) / HEAD_DIM)
    ang = jnp.arange(seq_len, dtype=jnp.float32)[:, None] * inv_freq[None, :]
    cos = jnp.concatenate([jnp.cos(ang), jnp.cos(ang)], axis=-1)
    sin = jnp.concatenate([jnp.sin(ang), jnp.sin(ang)], axis=-1)
    tf = t.astype(jnp.float32)
    half = HEAD_DIM // 2
    rot = jnp.concatenate([-tf[..., half:], tf[..., :half]], axis=-1)
    return (tf * cos + rot * sin).astype(t.dtype)


def banded_causal_attention(q, k, v, back):
    *lead, L, hd = q.shape
    blk = back
    n = L // blk
    qb = q.reshape(*lead, n, blk, hd).astype(jnp.float32)
    kb = k.reshape(*lead, n, blk, hd).astype(jnp.float32)
    vb = v.reshape(*lead, n, blk, hd).astype(jnp.float32)
    zero = jnp.zeros_like(kb[..., :1, :, :])
    kk = jnp.concatenate([jnp.concatenate([zero, kb[..., :-1, :, :]], axis=-3), kb], axis=-2)
    vv = jnp.concatenate([jnp.concatenate([zero, vb[..., :-1, :, :]], axis=-3), vb], axis=-2)
    s = jnp.einsum('...nqd,...nkd->...nqk', qb, kk) * (hd ** -0.5)
    qi = jnp.arange(blk)[:, None]
    kj = jnp.arange(2 * blk)[None, :]
    dist = qi + blk - kj
    band = (dist >= 0) & (dist <= back)
    in_range = (jnp.arange(n)[:, None, None] > 0) | (kj >= blk)[None]
    s = jnp.where(band[None] & in_range, s, -jnp.inf)
    m = jnp.max(s, axis=-1, keepdims=True)
    p = jnp.exp(s - m)
    den = jnp.sum(p, axis=-1, keepdims=True)
    out = jnp.einsum('...nqk,...nkd->...nqd', p, vv) / den
    lse = (m + jnp.log(den))[..., 0]
    return out.reshape(*lead, L, hd), lse.reshape(*lead, L)


def dilated_window_attention(q, k, v, window, dilation):
    B, H, S, hd = q.shape
    back = window // dilation
    span = dilation * back
    s_pad = -(-S // span) * span

    def to_residue(t):
        t = jnp.pad(t, ((0, 0), (0, 0), (0, s_pad - S), (0, 0)))
        t = t.reshape(B, H, s_pad // dilation, dilation, hd)
        return jnp.swapaxes(t, 2, 3)

    out, lse = banded_causal_attention(to_residue(q), to_residue(k), to_residue(v), back)
    out = jnp.swapaxes(out, 2, 3).reshape(B, H, s_pad, hd)[:, :, :S]
    lse = jnp.swapaxes(lse, 2, 3).reshape(B, H, s_pad)[:, :, :S]
    return out, lse


def hgrn2_chunked(q, logf, k, v):
    B, H, S, dk = q.shape
    dv = v.shape[-1]
    n = S // HGRN_CHUNK

    def chunks(t):
        return jnp.moveaxis(t.reshape(B, H, n, HGRN_CHUNK, t.shape[-1]), 2, 0)

    causal = jnp.tril(jnp.ones((HGRN_CHUNK, HGRN_CHUNK), dtype=bool))

    def step(state, inp):
        qc, gc, kc, vc = inp
        b = jnp.cumsum(gc, axis=-2)
        o_inter = jnp.einsum('bhtk,bhkv->bhtv', qc * jnp.exp(b), state)
        rel = b[:, :, :, None, :] - b[:, :, None, :, :]
        decay = jnp.exp(jnp.where(causal[:, :, None], rel, -jnp.inf))
        a = jnp.einsum('bhtk,bhtsk,bhsk->bhts', qc, decay, kc)
        o = o_inter + jnp.einsum('bhts,bhsv->bhtv', a, vc)
        b_last = b[:, :, -1:, :]
        new_state = jnp.exp(b_last[:, :, 0, :])[..., None] * state + jnp.einsum(
            'bhsk,bhsv->bhkv', kc * jnp.exp(b_last - b), vc)
        return new_state, o

    state0 = jnp.zeros((B, H, dk, dv), jnp.float32)
    _, o = lax.scan(step, state0, (chunks(q), chunks(logf), chunks(k), chunks(v)))
    return jnp.moveaxis(o, 0, 2).reshape(B, H, S, dv)


def split_columns(proj):
    sizes = [ATTN_QKV_WIDTH] * 3 + [HGRN_WIDTH] * 4 + [D_MODEL] * 2
    points = [int(p) for p in np.cumsum(sizes)[:-1]]
    return jnp.split(proj, points, axis=-1)


def setup_inputs(seed: int = 0) -> dict:
    key = jax.random.key(seed)
    ks = jax.random.split(key, 16)
    f32 = jnp.float32

    def w(k, shape, fan_in):
        return jax.random.normal(k, shape, f32) * (fan_in ** -0.5)

    def gain(k, shape):
        return 1.0 + 0.02 * jax.random.normal(k, shape, f32)

    return {
        "x": jax.random.normal(ks[0], (BATCH, SEQ, D_MODEL), f32),
        "w_in": w(ks[1], (DEPTH, D_MODEL, IN_COLS), D_MODEL),
        "w_attn_branch": w(ks[2], (DEPTH, ATTN_OUT_WIDTH, D_MODEL), ATTN_OUT_WIDTH),
        "w_hgrn_branch": w(ks[3], (DEPTH, HGRN_HEADS * HGRN_VAL_DIM, D_MODEL), HGRN_HEADS * HGRN_VAL_DIM),
        "w_mix_out": w(ks[4], (DEPTH, D_MODEL, D_MODEL), D_MODEL),
        "hgrn_lower_bounds": 0.1 * jax.random.normal(ks[5], (DEPTH + 1, HGRN_WIDTH), f32),
        "hgrn_norm_gain": gain(ks[6], (DEPTH, HGRN_HEADS * HGRN_VAL_DIM)),
        "norm_mix_pre": gain(ks[7], (DEPTH, D_MODEL)),
        "norm_mix_post": gain(ks[8], (DEPTH, D_MODEL)),
        "w_ffn_gate_up": w(ks[9], (DEPTH, D_MODEL, 2 * D_FF), D_MODEL),
        "w_ffn_down": w(ks[10], (DEPTH, D_FF, D_MODEL), D_FF),
        "norm_ffn_pre": gain(ks[11], (DEPTH, D_MODEL)),
        "norm_ffn_post": gain(ks[12], (DEPTH, D_MODEL)),
    }


def reference(x, w_in, w_attn_branch, w_hgrn_branch, w_mix_out, hgrn_lower_bounds, hgrn_norm_gain,
              norm_mix_pre, norm_mix_post, w_ffn_gate_up, w_ffn_down, norm_ffn_pre, norm_ffn_post):
    B, S, _ = x.shape
    lower_bounds = jnp.cumsum(jax.nn.softmax(hgrn_lower_bounds.astype(jnp.float32), axis=0), axis=0)
    h = x
    for layer in range(DEPTH):
        a = rmsnorm(h, norm_mix_pre[layer])
        proj = a @ w_in[layer]
        q_a, k_a, v_a, q_r, f_r, i_r, g_r, gate_a, gate_r = split_columns(proj)

        def heads(t):
            return t.reshape(B, S, N_GROUPS, ATTN_HEADS_PER_GROUP, HEAD_DIM).transpose(2, 0, 3, 1, 4)
        qh = rotary(heads(q_a), S)
        kh = rotary(heads(k_a), S)
        vh = heads(v_a)
        outs, lses = [], []
        for g, (window, dilation) in enumerate(ATTN_GROUPS):
            o_g, lse_g = dilated_window_attention(qh[g], kh[g], vh[g], window, dilation)
            outs.append(o_g)
            lses.append(lse_g)
        mix_w = jax.nn.softmax(jnp.stack(lses, axis=0), axis=0)
        attn = jnp.einsum('gbhs,gbhsd->bshd', mix_w, jnp.stack(outs, axis=0))
        attn = attn.reshape(B, S, ATTN_OUT_WIDTH).astype(x.dtype)

        def rheads(t):
            return t.reshape(B, S, HGRN_HEADS, -1).transpose(0, 2, 1, 3).astype(jnp.float32)
        lb = lower_bounds[layer].reshape(HGRN_HEADS, HGRN_KEY_DIM)[None, :, None, :]
        q_h = jax.nn.silu(rheads(q_r))
        f = lb + (1.0 - lb) * jax.nn.sigmoid(rheads(f_r))
        r_out = hgrn2_chunked(q_h, jnp.log(f), 1.0 - f, rheads(i_r))
        r_out = r_out.transpose(0, 2, 1, 3)
        r_out = r_out * lax.rsqrt(jnp.mean(r_out * r_out, axis=-1, keepdims=True) + NORM_EPS)
        r_out = r_out * hgrn_norm_gain[layer].astype(jnp.float32).reshape(HGRN_HEADS, HGRN_VAL_DIM)
        r_out = (r_out.reshape(B, S, HGRN_HEADS * HGRN_VAL_DIM) * jax.nn.silu(g_r.astype(jnp.float32))).astype(x.dtype)

        y = jax.nn.sigmoid(gate_a) * (attn @ w_attn_branch[layer]) + jax.nn.sigmoid(gate_r) * (r_out @ w_hgrn_branch[layer])
        h = h + rmsnorm(y @ w_mix_out[layer], norm_mix_post[layer])

        a = rmsnorm(h, norm_ffn_pre[layer])
        gt, up = jnp.split(a @ w_ffn_gate_up[layer], [D_FF], axis=-1)
        ff = (jax.nn.silu(gt) * up) @ w_ffn_down[layer]
        h = h + rmsnorm(ff, norm_ffn_post[layer])
    return h
```

```python
import contextlib
import numpy as np
import ml_dtypes
import concourse.bass as bass
import concourse.mybir as mybir
from concourse.bass_utils import run_bass_kernel_spmd

F32 = mybir.dt.float32
BF16 = mybir.dt.bfloat16
ALU = mybir.AluOpType
AF = mybir.ActivationFunctionType

D = 2048
KT = 16
N = 2048
HALO = 2048
HR = 512
NCH = (HR + N) // 64
DFF = 5632
NFB = DFF // 128
DIL = (1, 4, 16)
EPS = 1e-6
QA0, KA0, VA0, QR0, FR0, IR0, GR0, GA0, GB0 = 0, 3072, 6144, 9216, 10240, 11264, 12288, 13312, 15360
NDS = 8
ARENA = 211968


def ssl(start, n, step):
    return slice(start, start + (n - 1) * step + 1, step)


class Sched:
    ENGS = ("pe", "act", "dve", "pool", "sp")

    def __init__(self, nc, stack):
        self.nc = nc
        self.streams = {e: [] for e in self.ENGS}
        self.count = {e: 0 for e in self.ENGS}
        self.semobj = {}
        for e in ("pe", "act", "dve", "pool"):
            self.semobj["es_" + e] = stack.enter_context(nc.semaphore("es_" + e))
        self.dval = {}
        self.didx = {}
        for q in ("sp", "pool"):
            self.dval[q] = [0] * NDS
            self.didx[q] = 0
            for i in range(NDS):
                self.semobj["ds_%s%d" % (q, i)] = stack.enter_context(nc.semaphore("ds_%s%d" % (q, i)))
        self.last_write = {}
        self.readers = {}
        self.waited = {e: {} for e in self.ENGS}

    def _mk_waits(self, eng, toks):
        need = {}
        for (sname, val, _e) in toks:
            if val > need.get(sname, 0):
                need[sname] = val
        waits = []
        w = self.waited[eng]
        for sname, val in need.items():
            if w.get(sname, 0) < val:
                w[sname] = val
                waits.append((self.semobj[sname], val))
        return waits

    def _collect(self, eng, reads, writes, is_dma):
        toks = []
        for k in reads:
            t = self.last_write.get(k)
            if t is not None:
                toks.append(t)
        for k in writes:
            t = self.last_write.get(k)
            if t is not None:
                toks.append(t)
            for t in self.readers.get(k, ()):
                toks.append(t)
        return toks

    def _record(self, tok, reads, writes):
        for k in reads:
            self.readers.setdefault(k, []).append(tok)
        for k in writes:
            self.last_write[k] = tok
            self.readers[k] = []

    def task(self, eng, fn, r=(), w=()):
        w = list(w) + [k for k in r if k.startswith("pb")]
        r = [k for k in r if not k.startswith("pb")]
        toks = self._collect(eng, r, w, False)
        waits = self._mk_waits(eng, toks)
        self.count[eng] += 1
        sname = "es_" + eng
        tok = (sname, self.count[eng], eng)
        self.streams[eng].append((waits, fn, self.semobj[sname], 1))
        self._record(tok, r, w)
        return tok

    def dma(self, q, fn, ndma, r=(), w=()):
        toks = self._collect(q, r, w, True)
        slot = self.didx[q] % NDS
        self.didx[q] += 1
        sname = "ds_%s%d" % (q, slot)
        prev = self.dval[q][slot]
        if prev > 0:
            toks.append((sname, prev, "dma"))
        waits = self._mk_waits(q, toks)
        self.dval[q][slot] = prev + 16 * ndma
        tok = (sname, self.dval[q][slot], "dma")
        self.streams[q].append((waits, fn, self.semobj[sname], 16 * ndma))
        self._record(tok, r, w)
        return tok

    def barrier(self):
        toks = []
        for e in ("pe", "act", "dve", "pool"):
            if self.count[e] > 0:
                toks.append(("es_" + e, self.count[e], e))
        for i in range(NDS):
            if self.dval["sp"][i] > 0:
                toks.append(("ds_sp%d" % i, self.dval["sp"][i], "dma"))
        for e in self.ENGS:
            waits = self._mk_waits(e, toks)
            if waits:
                self.streams[e].append((waits, None, None, 0))

    def finish(self, eng, toks):
        waits = self._mk_waits(eng, toks)
        self.streams[eng].append((waits, None, None, 0))

    def check(self):
        val = {}
        pos = {e: 0 for e in self.ENGS}
        name = {id(v): k for k, v in self.semobj.items()}
        progress = True
        while progress:
            progress = False
            for e in self.ENGS:
                st = self.streams[e]
                while pos[e] < len(st):
                    waits, fn, sem, inc = st[pos[e]]
                    if any(val.get(name[id(s)], 0) < v for (s, v) in waits):
                        break
                    if fn is not None:
                        k = name[id(sem)]
                        val[k] = val.get(k, 0) + inc
                    pos[e] += 1
                    progress = True
        stuck = {e: (pos[e], len(self.streams[e])) for e in self.ENGS if pos[e] < len(self.streams[e])}
        if stuck:
            msg = []
            for e, (p, n) in stuck.items():
                waits = self.streams[e][p][0]
                msg.append("%s at %d/%d waits %s" % (e, p, n, [(name[id(s)], v, val.get(name[id(s)], 0)) for s, v in waits]))
            raise RuntimeError("DEADLOCK: " + "; ".join(msg))
        return {e: len(self.streams[e]) for e in self.ENGS}

    def emit(self):
        nc = self.nc
        streams = self.streams
        print("sched check:", self.check())

        def run(name, e):
            for (waits, fn, sem, inc) in streams[name]:
                for (s, v) in waits:
                    e.wait_ge(s, v)
                if fn is None:
                    continue
                if inc >= 16:
                    fn(e, lambda ins: ins.then_inc(sem, 16))
                else:
                    fn(e).then_inc(sem, 1)

        with nc.Block() as block:
            @block.tensor
            def _(e):
                run("pe", e)

            @block.scalar
            def _(e):
                run("act", e)

            @block.vector
            def _(e):
                run("dve", e)

            @block.gpsimd
            def _(e):
                run("pool", e)

            @block.sync
            def _(e):
                run("sp", e)


class Arena:
    def __init__(self, t, lo=0, hi=ARENA):
        self.t = t
        self.lo = lo
        self.top = lo
        self.hi = hi

    def region(self, lo, hi):
        return Arena(self.t, lo, hi)

    def alloc(self, free, dt, parts=128):
        n = int(np.prod(free))
        esz = 2 if dt == BF16 else 4
        nb = (n * esz + 63) // 64 * 64
        off = self.top
        self.top += nb
        assert self.top <= self.hi, ("arena overflow", self.top, self.hi)
        v = self.t[0:parts, off // 2: off // 2 + (n * esz) // 2]
        if dt == F32:
            v = v.bitcast(F32)
        if len(free) == 2:
            v = v.rearrange("p (a b) -> p a b", a=free[0], b=free[1])
        elif len(free) == 3:
            v = v.rearrange("p (a b c) -> p a b c", a=free[0], b=free[1], c=free[2])
        return v


class _Stop(Exception):
    pass


def build(dbg=False, stop=99):
    try:
        return _build(dbg, stop)
    except _Stop as e:
        return e.args[0]


def _build(dbg, stop):
    nc = bass.Bass("TRN2", target_bir_lowering=False)

    def din(name, shape, dt=F32):
        return nc.dram_tensor(name, list(shape), dt, kind="ExternalInput").ap()

    def dscr(name, shape, dt):
        return nc.dram_tensor(name, list(shape), dt, kind=("ExternalOutput" if dbg else "Internal")).ap()

    xe = din("xe", [HALO + N, D])
    w_in = din("w_in", [D, 17408])
    w_pa = din("w_pa", [1024, D])
    w_pr = din("w_pr", [1024, D])
    w_o = din("w_o", [D, D])
    w_gu = din("w_gu", [D, 2 * DFF])
    w_d = din("w_d", [DFF, D])
    gpreT_d = din("gpreT", [128, KT])
    gfpreT_d = din("gfpreT", [128, KT])
    gpost_d = din("gpost", [1, D])
    gfpost_d = din("gfpost", [1, D])
    hgainT_d = din("hgainT", [128, 8])
    lb0T_d = din("lb0T", [128, 8])
    lb1T_d = din("lb1T", [128, 8])
    ident_d = din("ident", [128, 128], BF16)
    rotm_d = din("rotm", [128, 128], BF16)
    amask_d = din("amask", [128, 2, 256], BF16)
    hmask_d = din("hmask", [64, 64], BF16)
    cos_d = din("cosT", [128, HALO + N])
    sin_d = din("sinT", [128, HALO + N])
    rmask_d = din("rmask", [128, HR + N])
    out_d = nc.dram_tensor("out", [N, D], F32, kind="ExternalOutput").ap()

    QT_d = dscr("QT_s", [24, 128, N], BF16)
    KT_d = dscr("KT_s", [24, 128, HALO + N], BF16)
    V_d = [dscr("V%d_s" % g, [8, 128, 16 + DIL[g], 128], BF16) for g in range(3)]
    qrT_d = dscr("qrT_s", [8, 128, N], BF16)
    fT_d = dscr("fT_s", [8, 128, HR + N], F32)
    grT_d = dscr("grT_s", [8, 128, N], BF16)
    iR_d = dscr("iR_s", [8, 2, 64, NCH // 2, 128], BF16)
    GaT_d = dscr("GaT_s", [16, 128, N], BF16)
    GbT_d = dscr("GbT_s", [16, 128, N], BF16)
    h1_d = dscr("h1_s", [N, D], F32)
    wgu_b = nc.dram_tensor("wgu_b", [NFB // 2, D, 512], BF16, kind="Internal").ap()
    wd_b = nc.dram_tensor("wd_b", [DFF, D], BF16, kind="Internal").ap()
    wo_b = nc.dram_tensor("wo_b", [D, D], BF16, kind="Internal").ap()
    if dbg:
        AT_dbg = dscr("AT_s", [128, 8, N], BF16)
        RT_dbg = dscr("RT_s", [128, 8, N], BF16)
        yT_dbg = dscr("yT_s", [128, KT, N], BF16)

    with contextlib.ExitStack() as st:
        S = Sched(nc, st)
        T = S.task
        arena_t = st.enter_context(nc.sbuf_tensor("arena", [128, ARENA // 2], BF16))
        A = Arena(arena_t)
        pbt = [st.enter_context(nc.psum_tensor("pb%d" % i, [128, 512], F32)) for i in range(8)]
        pb = [t[:] for t in pbt]
        pbh = [t[:].bitcast(BF16) for t in pbt]

        ident = A.alloc([128], BF16)
        rotm = A.alloc([128], BF16)
        ones = A.alloc([128], BF16)
        amask = A.alloc([2, 256], BF16)
        hmask = A.alloc([64], BF16, parts=64)
        gpreT = A.alloc([KT], F32)
        gfpreT = A.alloc([KT], F32)
        hgainT = A.alloc([8], F32)
        lbT = A.alloc([8], F32)
        omlT = A.alloc([8], F32)
        lbtmp = A.alloc([16], F32)
        epsb = A.alloc([1], F32)
        stats = A.alloc([128], F32)
        O_B = A.top
        bigB = A.alloc([KT, N], BF16)
        O_W = A.top
        wsb = [A.alloc([KT, 512], BF16) for _ in range(3)]
        O_A = A.top
        bigA = A.alloc([KT, N], BF16)
        O_T = A.top

        def stop_here(k):
            if stop == k:
                S.barrier()
                S.emit()
                raise _Stop(nc)

        def sp_load(dst, src, wkeys, rkeys=()):
            return S.dma("sp", lambda e, inc: inc(e.dma_start(out=dst, in_=src)), 1, r=list(rkeys), w=list(wkeys))

        sp_load(ident, ident_d, ["ident"])
        sp_load(rotm, rotm_d, ["rotm"])
        sp_load(amask, amask_d, ["amask"])
        sp_load(hmask, hmask_d, ["hmask"])
        sp_load(gpreT, gpreT_d, ["gpreT"])
        sp_load(gfpreT, gfpreT_d, ["gfpreT"])
        sp_load(hgainT, hgainT_d, ["hgainT"])
        sp_load(lbtmp[:, 0:8], lb0T_d, ["lbtmp0"])
        sp_load(lbtmp[:, 8:16], lb1T_d, ["lbtmp1"])
        T("dve", lambda e: e.memset(ones, 1.0), w=["ones"])
        T("dve", lambda e: e.memset(epsb, EPS), w=["epsb"])
        T("dve", lambda e: e.tensor_tensor(out=lbtmp[:, 0:8], in0=lbtmp[:, 0:8], in1=lbtmp[:, 8:16], op=ALU.subtract),
          r=["lbtmp0", "lbtmp1"], w=["lbtmp0"])
        T("act", lambda e: e.activation(out=lbT, in_=lbtmp[:, 0:8], func=AF.Sigmoid), r=["lbtmp0"], w=["lbT"])
        T("dve", lambda e: e.tensor_scalar(out=omlT, in0=lbT, scalar1=-1.0, scalar2=1.0, op0=ALU.mult, op1=ALU.add),
          r=["lbT"], w=["omlT"])

        wjob = [0]

        def wload(parts):
            slot = wjob[0] % 3
            wjob[0] += 1

            def fn(e, inc):
                for (k0, nk, c0, ncol, src) in parts:
                    inc(e.dma_start(out=wsb[slot][:, k0:k0 + nk, c0:c0 + ncol],
                                    in_=src.rearrange("(kt p) c -> p kt c", p=128)))
            S.dma("pool", fn, len(parts), w=["w%d" % slot])
            return slot

        conv = []
        for jb in range(NFB // 2):
            conv.append((wgu_b[jb][:, 0:256], w_gu[:, jb * 256:(jb + 1) * 256], "cvgu%d" % jb))
            conv.append((wgu_b[jb][:, 256:512], w_gu[:, DFF + jb * 256:DFF + (jb + 1) * 256], "cvgu%d" % jb))
        for i in range(11):
            conv.append((wd_b[i * 512:(i + 1) * 512, :], w_d[i * 512:(i + 1) * 512, :], "cvd%d" % i))
        for i in range(4):
            conv.append((wo_b[i * 512:(i + 1) * 512, :], w_o[i * 512:(i + 1) * 512, :], "cvo%d" % i))
        CVGU = ["cvgu%d" % jb for jb in range(NFB // 2)]
        CVD = ["cvd%d" % i for i in range(11)]
        CVO = ["cvo%d" % i for i in range(4)]
        conv_state = {"i": 0, "last": {}}

        def conv_some(k):
            for _ in range(k):
                if conv_state["i"] >= len(conv):
                    return
                dst, src, key = conv[conv_state["i"]]
                conv_state["i"] += 1
                S.dma("pool", lambda e, inc, dst=dst, src=src: inc(e.dma_start(out=dst, in_=src)), 1, w=[key + "_%d" % conv_state["i"]])
                conv_state["last"].setdefault(key, []).append(key + "_%d" % conv_state["i"])

        def cvkeys(keys):
            out = []
            for k in keys:
                out += conv_state["last"].get(k, [])
            return out

        def wload_b(parts, rkeys):
            slot = wjob[0] % 3
            wjob[0] += 1

            def fn(e, inc):
                for (k0, nk, c0, ncol, src) in parts:
                    inc(e.dma_start(out=wsb[slot][:, k0:k0 + nk, c0:c0 + ncol],
                                    in_=src.rearrange("(kt p) c -> p kt c", p=128)))
            S.dma("sp", fn, len(parts), r=rkeys, w=["w%d" % slot])
            return slot

        def run_jobs(jobs, nconv=0, loader=None):
            loader = loader or wload
            slots = {}
            n = len(jobs)
            for i in range(min(2, n)):
                slots[i] = loader(jobs[i][0])
            for i in range(n):
                if i + 2 < n:
                    slots[i + 2] = loader(jobs[i + 2][0])
                if nconv:
                    conv_some(nconv)
                jobs[i][1](slots[i])

        stat_i = [0]

        def stat_col():
            c = stat_i[0] % 128
            stat_i[0] += 1
            return stats[:, c:c + 1], "st%d" % c

        def norm_T(src_fn, ntiles, gT, gkey, dstT, dkey_fn, L):
            for t in range(ntiles):
                b = t % 2
                xt, xs = L["xt"][b], L["xs"][b]
                sp_load(xt, src_fn(t), ["xt%d" % b])
                c0, k0 = stat_col()
                c1, k1 = stat_col()
                T("act", lambda e, xt=xt, c0=c0: e.activation(out=L["junk"], in_=xt, func=AF.Square, accum_out=c0),
                  r=["xt%d" % b], w=["junk", k0])
                T("act", lambda e, c0=c0, c1=c1: e.activation(out=c1, in_=c0, func=AF.Sqrt, scale=1.0 / D, bias=epsb),
                  r=[k0, "epsb"], w=[k1])
                T("dve", lambda e, c0=c0, c1=c1: e.reciprocal(out=c0, in_=c1), r=[k1], w=[k0])
                T("act", lambda e, xt=xt, xs=xs, c0=c0: e.activation(out=xs, in_=xt, func=AF.Copy, scale=c0),
                  r=["xt%d" % b, k0], w=["xs%d" % b])
                for half in range(2):
                    pi = 6 + half

                    def tr(e, xs=xs, half=half, pi=pi):
                        ins = None
                        for k in range(8):
                            kt = half * 8 + k
                            ins = e.transpose(out=pbh[pi][:, k * 128:(k + 1) * 128], in_=xs[:, kt * 128:(kt + 1) * 128],
                                              identity=ident)
                        return ins
                    T("pe", tr, r=["xs%d" % b, "ident"], w=["pb%d" % pi])
                    T("dve", lambda e, half=half, t=t, pi=pi: e.tensor_tensor(
                        out=dstT[:, half * 8:(half + 1) * 8, t * 128:(t + 1) * 128],
                        in0=pbh[pi].rearrange("p (k n) -> p k n", k=8),
                        in1=gT[:, half * 8:(half + 1) * 8].unsqueeze(2).broadcast_to([128, 8, 128]),
                        op=ALU.mult), r=["pb%d" % pi, gkey], w=[dkey_fn(t)])

        aT = bigA
        RB = A.region(O_B, O_W)
        RT_ = A.region(O_T, ARENA)
        NL = {"xt": [RB.alloc([D], F32) for _ in range(2)], "xs": [RB.alloc([D], BF16) for _ in range(2)],
              "junk": RB.alloc([D], BF16)}
        stgv = [RB.alloc([8, 512], BF16) for _ in range(2)]
        stg = [RB.alloc([N], BF16) for _ in range(3)]
        stgf = [RB.alloc([N], F32) for _ in range(1)]
        cosS = RT_.alloc([N], F32)
        sinS = RT_.alloc([N], F32)
        qb = [RT_.alloc([512], BF16) for _ in range(2)]
        t1 = [RT_.alloc([512], F32) for _ in range(2)]
        t2 = [RT_.alloc([512], F32) for _ in range(2)]

        ctr = {"pb": 0, "stg": 0, "stgf": 0, "stgv": 0, "q": 0, "ev": 0, "scr": 0}

        def scr_store(dst, src, rkeys):
            ctr["scr"] += 1
            return sp_load(dst, src, ["scr%d" % ctr["scr"]], rkeys)

        def akeys(lo, hi):
            return ["aT%d" % c for c in range(lo // 512, (hi - 1) // 512 + 1)]

        def f_gemm(slot, nkt, tok_lo, tok_hi, rhsT, rkeys_fn, epi):
            pending = []
            for cb in range(4):
                lo = tok_lo
                while lo < tok_hi:
                    n = min(512, tok_hi - lo)
                    p = ctr["pb"] % 4
                    ctr["pb"] += 1

                    def mm(e, p=p, cb=cb, lo=lo, n=n):
                        ins = None
                        for kt in range(nkt):
                            ins = e.matmul(pb[p][:, 0:n], lhsT=wsb[slot][:, kt, cb * 128:(cb + 1) * 128],
                                           rhs=rhsT[:, kt, lo:lo + n], start=(kt == 0), stop=(kt == nkt - 1))
                        return ins
                    T("pe", mm, r=["w%d" % slot] + rkeys_fn(lo, lo + n), w=["pb%d" % p])
                    for f in pending:
                        f()
                    pending = epi(p, cb, lo, n) or []
                    lo += n
            for f in pending:
                f()

        def proj_phase(halo):
            tok0 = 0 if halo else HALO
            sp_load(cosS, cos_d[:, tok0:tok0 + N], ["cos"])
            sp_load(sinS, sin_d[:, tok0:tok0 + N], ["sin"])
            norm_T(lambda t: xe[tok0 + t * 128: tok0 + (t + 1) * 128, :], 16, gpreT, "gpreT", aT,
                   lambda t: "aT%d" % (t // 4), NL)
            if halo:
                stop_here(1)
            jobs = []

            def fjob(col, compute):
                jobs.append(([(0, KT, 0, 512, w_in[:, col:col + 512])], compute))

            def rot_job(col, row_base, dst_d, lo_t, hi_t, dcol0):
                def comp(slot):
                    state = {}

                    def epi(p, cb, lo, n):
                        import os
                        if os.environ.get("KNOEPI", "0") == "1":
                            return []
                        kepi = os.environ.get("KEPI", "")
                        if cb != state.get("cb"):
                            state["cb"] = cb
                            state["s"] = ctr["stg"] % 3
                            ctr["stg"] += 1
                        s = state["s"]
                        qi = ctr["q"] % 2
                        ctr["q"] += 1
                        if kepi:
                            if "a" in kepi:
                                T("act", lambda e: e.activation(out=qb[qi][:, 0:n], in_=pb[p][:, 0:n], func=AF.Copy),
                                  r=["pb%d" % p], w=["qb%d" % qi])
                            if "d" in kepi:
                                T("dve", lambda e: e.tensor_tensor(out=t1[qi][:, 0:n], in0=pb[p][:, 0:n],
                                                                   in1=cosS[:, lo:lo + n], op=ALU.mult),
                                  r=["pb%d" % p, "cos"], w=["t1%d" % qi])
                            if "c" in kepi:
                                T("dve", lambda e: e.tensor_copy(out=stg[s][:, lo:lo + n], in_=t1[qi][:, 0:n]),
                                  r=["t1%d" % qi], w=["stg%d" % s])
                            if "s" in kepi and lo + n >= hi_t:
                                scr_store(dst_d[row_base + cb][:, dcol0 + lo_t:dcol0 + hi_t], stg[s][:, lo_t:hi_t], ["stg%d" % s])
                            return []
                        T("act", lambda e: e.activation(out=qb[qi][:, 0:n], in_=pb[p][:, 0:n], func=AF.Copy),
                          r=["pb%d" % p], w=["qb%d" % qi])
                        T("dve", lambda e: e.tensor_tensor(out=t1[qi][:, 0:n], in0=pb[p][:, 0:n],
                                                           in1=cosS[:, lo:lo + n], op=ALU.mult),
                          r=["pb%d" % p, "cos"], w=["t1%d" % qi])
                        p2 = 4 + qi

                        def deferred():
                            import os
                            if os.environ.get("KROT", "1") == "0":
                                T("dve", lambda e: e.tensor_copy(out=stg[s][:, lo:lo + n], in_=t1[qi][:, 0:n]),
                                  r=["t1%d" % qi], w=["stg%d" % s])
                                if lo + n >= hi_t:
                                    scr_store(dst_d[row_base + cb][:, dcol0 + lo_t:dcol0 + hi_t], stg[s][:, lo_t:hi_t], ["stg%d" % s])
                                return
                            T("pe", lambda e: e.matmul(pb[p2][:, 0:n], lhsT=rotm, rhs=qb[qi][:, 0:n], start=True, stop=True),
                              r=["rotm", "qb%d" % qi], w=["pb%d" % p2])
                            T("dve", lambda e: e.tensor_tensor(out=t2[qi][:, 0:n], in0=pb[p2][:, 0:n],
                                                               in1=sinS[:, lo:lo + n], op=ALU.mult),
                              r=["pb%d" % p2, "sin"], w=["t2%d" % qi])
                            T("dve", lambda e: e.tensor_tensor(out=stg[s][:, lo:lo + n], in0=t1[qi][:, 0:n],
                                                               in1=t2[qi][:, 0:n], op=ALU.add),
                              r=["t1%d" % qi, "t2%d" % qi], w=["stg%d" % s])
                            if lo + n >= hi_t:
                                scr_store(dst_d[row_base + cb][:, dcol0 + lo_t:dcol0 + hi_t], stg[s][:, lo_t:hi_t], ["stg%d" % s])
                        return [deferred]
                    f_gemm(slot, KT, lo_t, hi_t, aT, akeys, epi)
                fjob(col, comp)

            def act_job(col, func, row_base, dst_d, lo_t, hi_t, dcol0):
                def comp(slot):
                    state = {}

                    def epi(p, cb, lo, n):
                        if cb != state.get("cb"):
                            state["cb"] = cb
                            state["s"] = ctr["stg"] % 3
                            ctr["stg"] += 1
                        s = state["s"]
                        T("act", lambda e: e.activation(out=stg[s][:, lo:lo + n], in_=pb[p][:, 0:n], func=func),
                          r=["pb%d" % p], w=["stg%d" % s])
                        if lo + n >= hi_t:
                            scr_store(dst_d[row_base + cb][:, dcol0 + lo_t:dcol0 + hi_t], stg[s][:, lo_t:hi_t], ["stg%d" % s])
                    f_gemm(slot, KT, lo_t, hi_t, aT, akeys, epi)
                fjob(col, comp)

            def f_job(col, row_base, lo_t, hi_t, dcol0):
                def comp(slot):
                    def epi(p, cb, lo, n):
                        s = 0
                        qi = ctr["q"] % 2
                        ctr["q"] += 1
                        h = row_base + cb
                        T("act", lambda e: e.activation(out=t2[qi][:, 0:n], in_=pb[p][:, 0:n], func=AF.Sigmoid),
                          r=["pb%d" % p], w=["t2%d" % qi])
                        T("dve", lambda e: e.tensor_scalar(out=stgf[s][:, lo:lo + n], in0=t2[qi][:, 0:n],
                                                           scalar1=omlT[:, h:h + 1], scalar2=lbT[:, h:h + 1],
                                                           op0=ALU.mult, op1=ALU.add),
                          r=["t2%d" % qi, "omlT", "lbT"], w=["stgf%d" % s])
                        if lo + n >= hi_t:
                            scr_store(fT_d[h][:, dcol0 + lo_t:dcol0 + hi_t], stgf[s][:, lo_t:hi_t], ["stgf%d" % s])
                    f_gemm(slot, KT, lo_t, hi_t, aT, akeys, epi)
                fjob(col, comp)

            def t_gemm(slot, blocks, store):
                for i0 in range(0, len(blocks), 8):
                    grp = blocks[i0:i0 + 8]
                    sv = ctr["stgv"] % 2
                    ctr["stgv"] += 1
                    for i, (csl, keys) in enumerate(grp):
                        p = ctr["pb"] % 4
                        ctr["pb"] += 1

                        def mm(e, p=p, csl=csl):
                            ins = None
                            for kt in range(KT):
                                ins = e.matmul(pb[p][:, :], lhsT=aT[:, kt, csl], rhs=wsb[slot][:, kt, :],
                                               start=(kt == 0), stop=(kt == KT - 1))
                            return ins
                        T("pe", mm, r=["w%d" % slot] + keys, w=["pb%d" % p])
                        ev = ctr["ev"] % 2
                        ctr["ev"] += 1
                        if ev == 0:
                            T("act", lambda e, p=p, i=i, sv=sv: e.activation(out=stgv[sv][:, i, :], in_=pb[p][:, :], func=AF.Copy),
                              r=["pb%d" % p], w=["stgv%d" % sv])
                        else:
                            T("dve", lambda e, p=p, i=i, sv=sv: e.tensor_copy(out=stgv[sv][:, i, :], in_=pb[p][:, :]),
                              r=["pb%d" % p], w=["stgv%d" % sv])
                    store(sv, i0, len(grp))

            def v_job(g, half):
                d = DIL[g]
                Hg = 128 * d
                col = VA0 + g * 1024 + half * 512
                if halo:
                    blocks = [(ssl(N - Hg + r, 128, d), akeys(N - Hg, N)) for r in range(d)]
                    b0 = 0
                else:
                    blocks = []
                    for m in range(1, 16 // d + 1):
                        for r in range(d):
                            st0 = (m - 1) * 128 * d + r
                            blocks.append((ssl(st0, 128, d), akeys(st0, st0 + 127 * d + 1)))
                    b0 = d

                def comp(slot):
                    def store(sv, i0, n):
                        def fn(e, inc):
                            for hh in range(4):
                                inc(e.dma_start(out=V_d[g][half * 4 + hh][:, b0 + i0:b0 + i0 + n, :],
                                                in_=stgv[sv][:, 0:n, hh * 128:(hh + 1) * 128]))
                        ctr["scr"] += 1
                        S.dma("sp", fn, 4, r=["stgv%d" % sv], w=["scr%d" % ctr["scr"]])
                    t_gemm(slot, blocks, store)
                fjob(col, comp)

            def i_job(half):
                col = IR0 + half * 512
                ntile = (HR // 128) if halo else (N // 128)
                t0 = (N - HR) if halo else 0
                pair0 = 0 if halo else HR // 128
                blocks = [(slice(t0 + i * 128, t0 + (i + 1) * 128), akeys(t0 + i * 128, t0 + (i + 1) * 128))
                          for i in range(ntile)]

                def comp(slot):
                    def store(sv, i0, n):
                        def fn(e, inc):
                            for hh in range(4):
                                for par in range(2):
                                    inc(e.dma_start(out=iR_d[half * 4 + hh][par][:, pair0 + i0:pair0 + i0 + n, :],
                                                    in_=stgv[sv][par * 64:(par + 1) * 64, 0:n, hh * 128:(hh + 1) * 128]))
                        ctr["scr"] += 1
                        S.dma("sp", fn, 8, r=["stgv%d" % sv], w=["scr%d" % ctr["scr"]])
                    t_gemm(slot, blocks, store)
                fjob(col, comp)

            if halo:
                import os
                sel = os.environ.get("KJOBS", "k,v0,v1,v2,f,i").split(",")
                for g in [int(v) for v in os.environ.get("KG", "0,1,2").split(",")]:
                    Hg = 128 * DIL[g]
                    for half in range(2):
                        if "k" in sel:
                            rot_job(KA0 + g * 1024 + half * 512, g * 8 + half * 4, KT_d, N - Hg, N, 0)
                        if "v%d" % g in sel:
                            v_job(g, half)
                for half in range(2):
                    if "f" in sel:
                        f_job(FR0 + half * 512, half * 4, N - HR, N, HR - N)
                    if "i" in sel:
                        i_job(half)
            else:
                for g in range(3):
                    for half in range(2):
                        rot_job(QA0 + g * 1024 + half * 512, g * 8 + half * 4, QT_d, 0, N, 0)
                        rot_job(KA0 + g * 1024 + half * 512, g * 8 + half * 4, KT_d, 0, N, HALO)
                        v_job(g, half)
                for half in range(2):
                    act_job(QR0 + half * 512, AF.Silu, half * 4, qrT_d, 0, N, 0)
                    f_job(FR0 + half * 512, half * 4, 0, N, HR)
                    i_job(half)
                    act_job(GR0 + half * 512, AF.Silu, half * 4, grT_d, 0, N, 0)
                for q in range(4):
                    act_job(GA0 + q * 512, AF.Sigmoid, q * 4, GaT_d, 0, N, 0)
                    act_job(GB0 + q * 512, AF.Sigmoid, q * 4, GbT_d, 0, N, 0)
            run_jobs(jobs, nconv=(1 if halo else 2))

        proj_phase(True)
        stop_here(2)
        proj_phase(False)
        conv_some(len(conv))
        S.barrier()
        stop_here(3)

        AT = bigB[:, 0:8, :]
        RT = bigB[:, 8:16, :]
        A = A.region(O_W, ARENA)
        QTs = [A.alloc([N], BF16) for _ in range(2)]
        KTs = [A.alloc([HALO + N], BF16) for _ in range(2)]
        Vs = [A.alloc([32, 128], BF16) for _ in range(2)]
        Pt = [A.alloc([512], BF16) for _ in range(4)]
        acc = A.alloc([2, N], F32)
        rec = A.alloc([N], F32)
        scale = 1.0 / float(np.sqrt(128.0))
        pairi = 0
        pendq = []
        for h in range(8):
            for g in range(3):
                d = DIL[g]
                Hg = 128 * d
                nblk = 16 + d
                sl = (h * 3 + g) % 2
                gh = g * 8 + h
                sp_load(QTs[sl], QT_d[gh], ["QT%d" % sl])
                sp_load(KTs[sl][:, 0:Hg + N], KT_d[gh][:, HALO - Hg:HALO + N], ["KT%d" % sl])
                sp_load(Vs[sl][:, 0:nblk, :], V_d[g][h], ["V%d" % sl])
                units = [(n, r) for n in range(16 // d) for r in range(d)]
                for pi_ in range(0, 16, 2):
                    pr2 = units[pi_:pi_ + 2]
                    bS = pairi % 4
                    bO = 4 + pairi % 4
                    pslot = pairi % 4
                    pairi += 1
                    kS, kO, kP = "pb%d" % bS, "pb%d" % bO, "P%d" % pslot
                    P = Pt[pslot]
                    info = []
                    for (n, r) in pr2:
                        q0 = n * 128 * d + r
                        info.append((ssl(q0, 128, d), ssl(Hg + q0, 128, d), n * d + r, (n + 1) * d + r, 1 if n == 0 else 0))

                    def mmS(e, info=info, bS=bS, sl=sl):
                        ins = None
                        for j, (qs, kc, bp, bc, var) in enumerate(info):
                            e.matmul(pb[bS][:, j * 256:j * 256 + 128], lhsT=KTs[sl][:, qs], rhs=QTs[sl][:, qs], start=True, stop=True)
                            ins = e.matmul(pb[bS][:, j * 256 + 128:j * 256 + 256], lhsT=KTs[sl][:, kc], rhs=QTs[sl][:, qs],
                                           start=True, stop=True)
                        return ins
                    T("pe", mmS, r=["QT%d" % sl, "KT%d" % sl], w=[kS])
                    T("act", lambda e, P=P, bS=bS: e.activation(out=P, in_=pb[bS], func=AF.Exp, scale=scale), r=[kS], w=[kP])
                    if info[0][4] == info[1][4]:
                        var = info[0][4]
                        T("pool", lambda e, P=P, var=var: e.tensor_tensor(
                            out=P.rearrange("p (a b) -> p a b", a=2), in0=P.rearrange("p (a b) -> p a b", a=2),
                            in1=amask[:, var:var + 1, :].broadcast_to([128, 2, 256]), op=ALU.mult), r=[kP, "amask"], w=[kP])
                    else:
                        def mk(e, P=P, info=info):
                            ins = None
                            for j in range(2):
                                ins = e.tensor_tensor(out=P[:, j * 256:(j + 1) * 256], in0=P[:, j * 256:(j + 1) * 256],
                                                      in1=amask[:, info[j][4], :], op=ALU.mult)
                            return ins
                        T("pool", mk, r=[kP, "amask"], w=[kP])

                    def pv(info=info, bO=bO, P=P, sl=sl, kO=kO, kP=kP, g=g):
                        def mmO(e):
                            ins = None
                            for j, (qs, kc, bp, bc, var) in enumerate(info):
                                o = j * 256
                                e.matmul(pb[bO][:, o:o + 128], lhsT=Vs[sl][:, bp, :], rhs=P[:, o:o + 128], start=True, stop=False)
                                e.matmul(pb[bO][:, o:o + 128], lhsT=Vs[sl][:, bc, :], rhs=P[:, o + 128:o + 256], start=False, stop=True)
                                e.matmul(pb[bO][:, o + 128:o + 256], lhsT=ones, rhs=P[:, o:o + 128], start=True, stop=False)
                                ins = e.matmul(pb[bO][:, o + 128:o + 256], lhsT=ones, rhs=P[:, o + 128:o + 256], start=False, stop=True)
                            return ins
                        T("pe", mmO, r=[kP, "V%d" % sl, "ones"], w=[kO])

                        def accum(e):
                            ins = None
                            for j, (qs, kc, bp, bc, var) in enumerate(info):
                                pOv = pb[bO][:, j * 256:(j + 1) * 256].rearrange("p (a q) -> p a q", a=2)
                                if g == 0:
                                    ins = e.tensor_copy(out=acc[:, :, qs], in_=pOv)
                                else:
                                    ins = e.tensor_tensor(out=acc[:, :, qs], in0=acc[:, :, qs], in1=pOv, op=ALU.add)
                            return ins
                        T("dve", accum, r=[kO, "acc"], w=["acc"])
                    pendq.append(pv)
                    if len(pendq) > 2:
                        pendq.pop(0)()
            while pendq:
                pendq.pop(0)()
            T("dve", lambda e: e.reciprocal(out=rec, in_=acc[:, 1, :]), r=["acc"], w=["rec"])
            T("dve", lambda e, h=h: e.tensor_tensor(out=AT[:, h, :], in0=acc[:, 0, :], in1=rec, op=ALU.mult),
              r=["acc", "rec"], w=["AT"])
        S.barrier()
        stop_here(4)

        A = A.region(O_W, ARENA)
        L = HR + N
        NHC = HR // 64
        rmask = A.alloc([L], F32)
        fS = A.alloc([L], F32)
        bS = A.alloc([L], F32)
        ebS = A.alloc([L], F32)
        enbS = A.alloc([L], F32)
        qrS = A.alloc([N], BF16)
        grS = A.alloc([N], BF16)
        qe = A.alloc([N], BF16)
        ke = A.alloc([L], BF16)
        iRs = A.alloc([2, NCH // 2, 128], BF16, parts=64)
        keTok = A.alloc([NCH, 128], BF16, parts=64)
        ATm2 = [A.alloc([8, 64], BF16, parts=64) for _ in range(2)]
        Sall = A.alloc([NCH + 1, 128], BF16)
        Tall = A.alloc([NCH, 128], F32)
        osb = A.alloc([512], F32)
        sqb = A.alloc([512], BF16)
        rtb = A.alloc([512], F32)
        sp_load(rmask, rmask_d, ["rmask"])
        for h in range(8):
            hk = lambda s: s
            sp_load(fS, fT_d[h], ["fS"])
            sp_load(qrS, qrT_d[h], ["qrS"])
            sp_load(grS, grT_d[h], ["grS"])
            S.dma("sp", lambda e, inc, h=h: [inc(e.dma_start(out=iRs[:, par, :, :], in_=iR_d[h][par])) for par in range(2)],
                  2, w=["iRs"])
            T("act", lambda e: e.activation(out=bS, in_=fS, func=AF.Ln), r=["fS"], w=["bS"])
            T("dve", lambda e: e.tensor_tensor_scan(out=ebS, data0=rmask, data1=bS, initial=0.0, op0=ALU.mult, op1=ALU.add),
              r=["bS", "rmask"], w=["ebS"])
            T("act", lambda e: e.activation(out=bS, in_=ebS, func=AF.Exp), r=["ebS"], w=["bS"])
            T("act", lambda e: e.activation(out=enbS, in_=ebS, func=AF.Exp, scale=-1.0), r=["ebS"], w=["enbS"])
            T("dve", lambda e: e.tensor_tensor(out=qe, in0=qrS, in1=bS[:, HR:L], op=ALU.mult), r=["qrS", "bS"], w=["qe"])
            T("dve", lambda e: e.tensor_scalar(out=fS, in0=fS, scalar1=-1.0, scalar2=1.0, op0=ALU.mult, op1=ALU.add),
              r=["fS"], w=["fS"])
            T("dve", lambda e: e.tensor_tensor(out=ke, in0=fS, in1=enbS, op=ALU.mult), r=["fS", "enbS"], w=["ke"])
            T("dve", lambda e: e.memset(Sall[:, 0, :], 0.0), w=["S0"])
            pend_o = []
            for gi in range(NCH // 8):
                own = gi * 8 >= NHC
                cs = list(range(gi * 8, gi * 8 + 8))

                def trk(e, cs=cs):
                    ins = None
                    for j, c in enumerate(cs):
                        ins = e.transpose(out=pbh[0][0:64, j * 128:(j + 1) * 128], in_=ke[:, c * 64:(c + 1) * 64], identity=ident)
                    return ins
                T("pe", trk, r=["ke", "ident"], w=["pb0"])
                T("act", lambda e, gi=gi: e.activation(out=keTok[:, gi * 8:gi * 8 + 8, :],
                                                       in_=pbh[0][0:64, :].rearrange("p (a b) -> p a b", a=8), func=AF.Copy),
                  r=["pb0"], w=["keTok"])
                for hb in range(2):
                    def mmU(e, cs=cs, hb=hb):
                        ins = None
                        for j in range(4):
                            c = cs[hb * 4 + j]
                            ins = e.matmul(pb[1 + hb][:, j * 128:(j + 1) * 128], lhsT=keTok[:, c, :],
                                           rhs=iRs[:, c % 2, c // 2, :], start=True, stop=True)
                        return ins
                    T("pe", mmU, r=["keTok", "iRs"], w=["pb%d" % (1 + hb)])
                if own:
                    def mmA(e, cs=cs):
                        ins = None
                        for j, c in enumerate(cs):
                            ins = e.matmul(pb[3][0:64, j * 64:(j + 1) * 64], lhsT=ke[:, c * 64:(c + 1) * 64],
                                           rhs=qe[:, (c - NHC) * 64:(c - NHC + 1) * 64], start=True, stop=True)
                        return ins
                    T("pe", mmA, r=["ke", "qe"], w=["pb3"])
                    ATm = ATm2[gi % 2]
                    T("dve", lambda e, ATm=ATm: e.tensor_tensor(out=ATm, in0=pb[3][0:64, :].rearrange("p (a b) -> p a b", a=8),
                                                                in1=hmask.unsqueeze(1).broadcast_to([64, 8, 64]), op=ALU.mult),
                      r=["pb3", "hmask"], w=["ATm%d" % (gi % 2)])
                while pend_o:
                    pend_o.pop(0)()
                for j, c in enumerate(cs):
                    Uc = pb[1 + j // 4][:, (j % 4) * 128:(j % 4 + 1) * 128]
                    ku = "pb%d" % (1 + j // 4)
                    if c == 0:
                        T("dve", lambda e, Uc=Uc: e.tensor_copy(out=Tall[:, 0, :], in_=Uc), r=[ku], w=["T0"])
                    else:
                        dlp = bS[:, c * 64 - 1:c * 64]
                        T("dve", lambda e, Uc=Uc, c=c, dlp=dlp: e.scalar_tensor_tensor(
                            out=Tall[:, c, :], in0=Tall[:, c - 1, :], scalar=dlp, in1=Uc, op0=ALU.mult, op1=ALU.add),
                          r=[ku, "T%d" % (c - 1), "bS"], w=["T%d" % c])
                    dl = bS[:, c * 64 + 63:c * 64 + 64]
                    T("act", lambda e, c=c, dl=dl: e.activation(out=Sall[:, c + 1, :], in_=Tall[:, c, :], func=AF.Copy, scale=dl),
                      r=["T%d" % c, "bS"], w=["S%d" % (c + 1)])
                if own:
                    def outg(cs=cs, gi=gi, h=h, ATm=ATm2[gi % 2]):
                        tok0 = (cs[0] - NHC) * 64
                        pO = pb[4 + gi % 2]
                        kO = "pb%d" % (4 + gi % 2)

                        def mmO(e):
                            ins = None
                            for j, c in enumerate(cs):
                                e.matmul(pO[:, j * 64:(j + 1) * 64], lhsT=iRs[:, c % 2, c // 2, :], rhs=ATm[:, j, :],
                                         start=True, stop=False)
                                ins = e.matmul(pO[:, j * 64:(j + 1) * 64], lhsT=Sall[:, c, :],
                                               rhs=qe[:, (c - NHC) * 64:(c - NHC + 1) * 64], start=False, stop=True)
                            return ins
                        T("pe", mmO, r=["iRs", "ATm%d" % (gi % 2), "qe"] + ["S%d" % c for c in cs], w=[kO])
                        T("act", lambda e: e.activation(out=osb, in_=pO, func=AF.Copy), r=[kO], w=["osb"])
                        T("act", lambda e: e.activation(out=sqb, in_=pO, func=AF.Square), r=[kO], w=["sqb"])
                        T("pe", lambda e: e.matmul(pb[6], lhsT=ones, rhs=sqb, start=True, stop=True), r=["sqb", "ones"], w=["pb6"])
                        T("act", lambda e: e.activation(out=rtb, in_=pb[6], func=AF.Sqrt, scale=1.0 / 128, bias=epsb),
                          r=["pb6", "epsb"], w=["rtb"])
                        T("dve", lambda e: e.reciprocal(out=rtb, in_=rtb), r=["rtb"], w=["rtb"])
                        T("dve", lambda e: e.tensor_tensor(out=osb, in0=osb, in1=rtb, op=ALU.mult), r=["osb", "rtb"], w=["osb"])
                        T("dve", lambda e: e.scalar_tensor_tensor(out=RT[:, h, tok0:tok0 + 512], in0=osb, scalar=hgainT[:, h:h + 1],
                                                                  in1=grS[:, tok0:tok0 + 512], op0=ALU.mult, op1=ALU.mult),
                          r=["osb", "hgainT", "grS"], w=["RT"])
                    pend_o.append(outg)
            while pend_o:
                pend_o.pop(0)()
        if dbg:
            sp_load(AT_dbg, AT, ["dbgAT"], ["AT"])
            sp_load(RT_dbg, RT, ["dbgRT"], ["RT"])
        S.barrier()
        stop_here(5)

        yT = bigA
        A = A.region(O_T, ARENA)
        GaS = [A.alloc([N], BF16) for _ in range(2)]
        GbS = [A.alloc([N], BF16) for _ in range(2)]
        u1 = [A.alloc([512], F32) for _ in range(2)]
        u2 = [A.alloc([512], F32) for _ in range(2)]
        jobs = []
        cnt5 = [0]
        for ct in range(4):
            def comp(slot, ct=ct):
                for fb in range(4):
                    f = ct * 4 + fb
                    gs = f % 2
                    sp_load(GaS[gs], GaT_d[f], ["Ga%d" % gs])
                    sp_load(GbS[gs], GbT_d[f], ["Gb%d" % gs])
                    for tc in range(4):
                        i = cnt5[0] % 2
                        cnt5[0] += 1
                        pa, pr = pb[i * 2], pb[i * 2 + 1]
                        ka, kr = "pb%d" % (i * 2), "pb%d" % (i * 2 + 1)

                        def mm(e, pa=pa, pr=pr, fb=fb, tc=tc):
                            for kt in range(8):
                                e.matmul(pa, lhsT=wsb[slot][:, kt, fb * 128:(fb + 1) * 128], rhs=AT[:, kt, tc * 512:(tc + 1) * 512],
                                         start=(kt == 0), stop=(kt == 7))
                            ins = None
                            for kt in range(8):
                                ins = e.matmul(pr, lhsT=wsb[slot][:, 8 + kt, fb * 128:(fb + 1) * 128],
                                               rhs=RT[:, kt, tc * 512:(tc + 1) * 512], start=(kt == 0), stop=(kt == 7))
                            return ins
                        T("pe", mm, r=["w%d" % slot, "AT", "RT"], w=[ka, kr])
                        T("dve", lambda e, i=i, pa=pa, gs=gs, tc=tc: e.tensor_tensor(
                            out=u1[i], in0=pa, in1=GaS[gs][:, tc * 512:(tc + 1) * 512], op=ALU.mult),
                          r=[ka, "Ga%d" % gs], w=["u1%d" % i])
                        T("dve", lambda e, i=i, pr=pr, gs=gs, tc=tc: e.tensor_tensor(
                            out=u2[i], in0=pr, in1=GbS[gs][:, tc * 512:(tc + 1) * 512], op=ALU.mult),
                          r=[kr, "Gb%d" % gs], w=["u2%d" % i])
                        T("dve", lambda e, i=i, f=f, tc=tc: e.tensor_tensor(
                            out=yT[:, f, tc * 512:(tc + 1) * 512], in0=u1[i], in1=u2[i], op=ALU.add),
                          r=["u1%d" % i, "u2%d" % i], w=["yT"])
            jobs.append(([(0, 8, 0, 512, w_pa[:, ct * 512:(ct + 1) * 512]), (8, 8, 0, 512, w_pr[:, ct * 512:(ct + 1) * 512])], comp))
        run_jobs(jobs)
        if dbg:
            sp_load(yT_dbg, yT, ["dbgyT"], ["yT"])
        S.barrier()
        stop_here(6)

        A = A.region(O_B, O_W)
        A6t = A.region(O_T, ARENA)
        zsb2 = [A.alloc([4, D], F32) for _ in range(2)]
        xt6 = A6t.alloc([D], F32)
        h1t = A6t.alloc([D], F32)
        gpb = A6t.alloc([D], F32)
        junk6 = A6t.alloc([D], BF16)
        sp_load(gpb, gpost_d.broadcast_to([128, D]), ["gpb"])
        jobs = []
        ev6 = [0]
        for tg in range(4):
            for ct in range(4):
                def comp(slot, tg=tg, ct=ct):
                    zsb = zsb2[tg % 2]
                    zk = "z%d_" % (tg % 2)
                    for tl in range(4):
                        p = ctr["pb"] % 4
                        ctr["pb"] += 1
                        tok = (tg * 4 + tl) * 128

                        def mm(e, p=p, tok=tok):
                            ins = None
                            for kt in range(KT):
                                ins = e.matmul(pb[p], lhsT=yT[:, kt, tok:tok + 128], rhs=wsb[slot][:, kt, :],
                                               start=(kt == 0), stop=(kt == KT - 1))
                            return ins
                        T("pe", mm, r=["w%d" % slot, "yT"], w=["pb%d" % p])
                        ev6[0] += 1
                        if ev6[0] % 2:
                            T("act", lambda e, p=p, tl=tl, zsb=zsb: e.activation(out=zsb[:, tl, ct * 512:(ct + 1) * 512], in_=pb[p], func=AF.Copy),
                              r=["pb%d" % p], w=[zk + str(tl)])
                        else:
                            T("dve", lambda e, p=p, tl=tl, zsb=zsb: e.tensor_copy(out=zsb[:, tl, ct * 512:(ct + 1) * 512], in_=pb[p]),
                              r=["pb%d" % p], w=[zk + str(tl)])
                    if ct == 3:
                        for tl in range(4):
                            row = (tg * 4 + tl) * 128
                            c0, k0 = stat_col()
                            c1, k1 = stat_col()
                            sp_load(xt6, xe[HALO + row:HALO + row + 128, :], ["xt6"])
                            T("act", lambda e, tl=tl, c0=c0, zsb=zsb: e.activation(out=junk6, in_=zsb[:, tl, :], func=AF.Square, accum_out=c0),
                              r=[zk + str(tl)], w=["junk6", k0])
                            T("act", lambda e, c0=c0, c1=c1: e.activation(out=c1, in_=c0, func=AF.Sqrt, scale=1.0 / D, bias=epsb),
                              r=[k0, "epsb"], w=[k1])
                            T("dve", lambda e, c0=c0, c1=c1: e.reciprocal(out=c0, in_=c1), r=[k1], w=[k0])
                            T("dve", lambda e, tl=tl, zsb=zsb: e.tensor_tensor(out=zsb[:, tl, :], in0=zsb[:, tl, :], in1=gpb, op=ALU.mult),
                              r=[zk + str(tl), "gpb"], w=[zk + str(tl)])
                            T("dve", lambda e, tl=tl, c0=c0, zsb=zsb: e.scalar_tensor_tensor(out=h1t, in0=zsb[:, tl, :], scalar=c0, in1=xt6,
                                                                                op0=ALU.mult, op1=ALU.add),
                              r=[zk + str(tl), k0, "xt6"], w=["h1t"])
                            sp_load(h1_d[row:row + 128, :], h1t, ["h1d"], ["h1t"])
                jobs.append(([(0, KT, 0, 512, wo_b[:, ct * 512:(ct + 1) * 512])], comp))
        run_jobs(jobs, loader=lambda parts: wload_b(parts, cvkeys(CVO)))
        S.barrier()
        stop_here(7)

        A1 = A.region(O_B, O_W)
        A2 = A.region(O_A, ARENA)
        NL7 = {"xt": [A1.alloc([D], F32) for _ in range(2)], "xs": [A1.alloc([D], BF16) for _ in range(2)],
               "junk": A1.alloc([D], BF16)}
        a2T = A1.alloc([KT, 512], BF16)
        h1r = A1.alloc([D], F32)
        gfb = A1.alloc([D], F32)
        hT = A2.alloc([NFB, 512], BF16)
        ffo = A2.alloc([4, D], F32)
        sg7 = [A2.alloc([512], F32) for _ in range(2)]
        sp_load(gfb, gfpost_d.broadcast_to([128, D]), ["gfb"])
        out_toks = []
        for tg in range(4):
            row0 = tg * 512
            norm_T(lambda t, row0=row0: h1_d[row0 + t * 128:row0 + (t + 1) * 128, :], 4, gfpreT, "gfpreT", a2T,
                   lambda t: "a2T", NL7)
            jobs = []
            for jb in range(NFB // 2):
                def comp(slot, jb=jb):
                    for fb in range(2):
                        i = ctr["q"] % 2
                        ctr["q"] += 1
                        pg, pu = pb[i * 2], pb[i * 2 + 1]
                        kg, ku = "pb%d" % (i * 2), "pb%d" % (i * 2 + 1)
                        ffb = jb * 2 + fb

                        def mm(e, pg=pg, pu=pu, fb=fb):
                            for kt in range(KT):
                                e.matmul(pg, lhsT=wsb[slot][:, kt, fb * 128:(fb + 1) * 128], rhs=a2T[:, kt, :],
                                         start=(kt == 0), stop=(kt == KT - 1))
                            ins = None
                            for kt in range(KT):
                                ins = e.matmul(pu, lhsT=wsb[slot][:, kt, 256 + fb * 128:256 + (fb + 1) * 128], rhs=a2T[:, kt, :],
                                               start=(kt == 0), stop=(kt == KT - 1))
                            return ins
                        T("pe", mm, r=["w%d" % slot, "a2T"], w=[kg, ku])
                        T("act", lambda e, i=i, pg=pg: e.activation(out=sg7[i], in_=pg, func=AF.Silu), r=[kg], w=["sg7%d" % i])
                        T("dve", lambda e, i=i, pu=pu, ffb=ffb: e.tensor_tensor(out=hT[:, ffb, :], in0=sg7[i], in1=pu, op=ALU.mult),
                          r=["sg7%d" % i, ku], w=["hT"])
                jobs.append(([(0, KT, 0, 512, wgu_b[jb])], comp))
            for cc in range(4):
                for (f0, nf) in ((0, 16), (16, 16), (32, 12)):
                    def comp(slot, cc=cc, f0=f0, nf=nf, tg=tg):
                        def mm(e):
                            ins = None
                            for fi in range(nf):
                                fbk = f0 + fi
                                for tl in range(4):
                                    ins = e.matmul(pb[4 + tl], lhsT=hT[:, fbk, tl * 128:(tl + 1) * 128], rhs=wsb[slot][:, fi, :],
                                                   start=(fbk == 0), stop=(fbk == NFB - 1))
                            return ins
                        T("pe", mm, r=["w%d" % slot, "hT"], w=["pb4", "pb5", "pb6", "pb7"])
                        if f0 + nf == NFB:
                            for tl in range(4):
                                if tl % 2:
                                    T("act", lambda e, tl=tl: e.activation(out=ffo[:, tl, cc * 512:(cc + 1) * 512], in_=pb[4 + tl], func=AF.Copy),
                                      r=["pb%d" % (4 + tl)], w=["ffo%d" % tl])
                                else:
                                    T("dve", lambda e, tl=tl: e.tensor_copy(out=ffo[:, tl, cc * 512:(cc + 1) * 512], in_=pb[4 + tl]),
                                      r=["pb%d" % (4 + tl)], w=["ffo%d" % tl])
                            if cc == 3:
                                for tl in range(4):
                                    row = tg * 512 + tl * 128
                                    c0, k0 = stat_col()
                                    c1, k1 = stat_col()
                                    sp_load(h1r, h1_d[row:row + 128, :], ["h1r"], ["h1d"])
                                    T("act", lambda e, tl=tl, c0=c0: e.activation(out=NL7["junk"], in_=ffo[:, tl, :], func=AF.Square, accum_out=c0),
                                      r=["ffo%d" % tl], w=["junk", k0])
                                    T("act", lambda e, c0=c0, c1=c1: e.activation(out=c1, in_=c0, func=AF.Sqrt, scale=1.0 / D, bias=epsb),
                                      r=[k0, "epsb"], w=[k1])
                                    T("dve", lambda e, c0=c0, c1=c1: e.reciprocal(out=c0, in_=c1), r=[k1], w=[k0])
                                    T("dve", lambda e, tl=tl: e.tensor_tensor(out=ffo[:, tl, :], in0=ffo[:, tl, :], in1=gfb, op=ALU.mult),
                                      r=["ffo%d" % tl, "gfb"], w=["ffo%d" % tl])
                                    T("dve", lambda e, tl=tl, c0=c0: e.scalar_tensor_tensor(out=ffo[:, tl, :], in0=ffo[:, tl, :], scalar=c0,
                                                                                        in1=h1r, op0=ALU.mult, op1=ALU.add),
                                      r=["ffo%d" % tl, k0, "h1r"], w=["ffo%d" % tl])
                                    out_toks.append(sp_load(out_d[row:row + 128, :], ffo[:, tl, :], ["outd"], ["ffo%d" % tl]))
                    jobs.append(([(0, nf, 0, 512, wd_b[f0 * 128:(f0 + nf) * 128, cc * 512:(cc + 1) * 512])], comp))
            run_jobs(jobs, loader=lambda parts: wload_b(parts, cvkeys(CVGU + CVD)))
        S.finish("sp", out_toks)
        S.emit()
    return nc


_CACHE = {}


def _consts():
    bf = ml_dtypes.bfloat16
    ident = np.eye(128, dtype=np.float32).astype(bf)
    rot = np.zeros((128, 128), np.float32)
    for m in range(64):
        rot[m + 64, m] = -1.0
    for m in range(64, 128):
        rot[m - 64, m] = 1.0
    kl = np.arange(128)[:, None]
    qi = np.arange(128)[None, :]
    am = np.concatenate([(kl >= qi), (kl <= qi)], axis=1).astype(np.float32)
    s = np.arange(64)[:, None]
    t = np.arange(64)[None, :]
    hm = (s <= t).astype(np.float32).astype(bf)
    rm = np.ones((128, HR + N), np.float32)
    rm[:, ::64] = 0.0
    return ident, rot.astype(bf), am, hm, rm


def _rope_tables(pos):
    inv_freq = (10000.0 ** (-np.arange(0, 128, 2, dtype=np.float32) / 128)).astype(np.float32)
    ang = pos.astype(np.float32)[None, :] * inv_freq[:, None]
    cos = np.concatenate([np.cos(ang), np.cos(ang)], axis=0).astype(np.float32)
    sin = np.concatenate([np.sin(ang), np.sin(ang)], axis=0).astype(np.float32)
    return np.ascontiguousarray(cos), np.ascontiguousarray(sin)


def make_in_maps(inputs, cores):
    bf = ml_dtypes.bfloat16
    x = np.asarray(inputs["x"], np.float32)
    ident, rot, am, hm, rm = _consts()

    def colT(v):
        return np.ascontiguousarray(np.asarray(v, np.float32).reshape(-1, 128).T)
    shared = {
        "w_in": np.ascontiguousarray(inputs["w_in"][0], dtype=np.float32),
        "w_pa": np.ascontiguousarray(inputs["w_attn_branch"][0], dtype=np.float32),
        "w_pr": np.ascontiguousarray(inputs["w_hgrn_branch"][0], dtype=np.float32),
        "w_o": np.ascontiguousarray(inputs["w_mix_out"][0], dtype=np.float32),
        "w_gu": np.ascontiguousarray(inputs["w_ffn_gate_up"][0], dtype=np.float32),
        "w_d": np.ascontiguousarray(inputs["w_ffn_down"][0], dtype=np.float32),
        "gpreT": colT(inputs["norm_mix_pre"][0]),
        "gfpreT": colT(inputs["norm_ffn_pre"][0]),
        "gpost": np.ascontiguousarray(np.asarray(inputs["norm_mix_post"][0], np.float32).reshape(1, D)),
        "gfpost": np.ascontiguousarray(np.asarray(inputs["norm_ffn_post"][0], np.float32).reshape(1, D)),
        "hgainT": colT(inputs["hgrn_norm_gain"][0]),
        "lb0T": colT(inputs["hgrn_lower_bounds"][0]),
        "lb1T": colT(inputs["hgrn_lower_bounds"][1]),
        "ident": ident, "rotm": rot, "hmask": hm, "rmask": rm,
    }
    maps = []
    for c in cores:
        b, j = c // 4, c % 4
        xe = np.zeros((HALO + N, D), np.float32)
        xe[HALO:] = x[b, j * N:(j + 1) * N]
        if j > 0:
            xe[:HALO] = x[b, (j - 1) * N:j * N]
        pos = np.arange(j * N - HALO, (j + 1) * N)
        cos, sin = _rope_tables(pos)
        amc = np.stack([am, am.copy()], axis=1)
        if j == 0:
            amc[:, 1, 0:128] = 0.0
        m = dict(shared)
        m.update({"xe": xe, "cosT": cos, "sinT": sin, "amask": np.ascontiguousarray(amc).astype(bf)})
        maps.append(m)
    return maps


def kernel(**inputs):
    if "nc" not in _CACHE:
        _CACHE["nc"] = build(False)
    nc = _CACHE["nc"]
    cores = list(range(8))
    maps = make_in_maps(inputs, cores)
    res = run_bass_kernel_spmd(nc, maps, core_ids=cores)
    out = np.empty((2, 4 * N, D), np.float32)
    for c in cores:
        out[c // 4, (c % 4) * N:(c % 4 + 1) * N] = res.results[c]["out"]
    return out
```
